# Optimizing a Trainium2 kernel written in Bass

```python
import math
import jax, jax.numpy as jnp
from jax import lax
import numpy as np

D_MODEL = 4096
BATCH = 1
SEQ = 8192
DEPTH = 4

GRID_W = 64
CTX_LEN = 256
CHUNK = 128
BRANCH_WIDTH = 1024
N_BRANCH = 3
A_GROUPS = 8
A_GDIM = BRANCH_WIDTH // A_GROUPS
B_HEADS = 4
B_DQK = 128
B_DV = BRANCH_WIDTH // B_HEADS
C_HEADS = 4
C_DH = 128
C_DV = 2 * C_DH
FFN_DIM = 4096
ADA_RANK = 256
N_SUB = 3
ROPE_THETA = 10000.0
EPS = 1e-6
SPLIT_SIZES = (2 * BRANCH_WIDTH,
               B_HEADS * B_DQK,
               B_HEADS * B_DQK,
               B_HEADS * B_DV,
               B_HEADS * B_DV,
               4 * B_HEADS,
               2 * C_HEADS * C_DH,
               2 * C_HEADS * C_DH,
               C_HEADS * C_DV,
               N_BRANCH * D_MODEL)
SPLIT_POINTS = tuple(sum(SPLIT_SIZES[:i + 1]) for i in range(len(SPLIT_SIZES) - 1))
IN_COLS = sum(SPLIT_SIZES)

kernel_name = 'hybrid_gated_mixer_dit'


def rmsnorm(x, g):
    xf = x.astype(jnp.float32)
    y = xf * lax.rsqrt(jnp.mean(xf * xf, axis=-1, keepdims=True) + EPS)
    return (y * g.astype(jnp.float32)).astype(x.dtype)


def layernorm(x, g):
    xf = x.astype(jnp.float32)
    xc = xf - jnp.mean(xf, axis=-1, keepdims=True)
    y = xc * lax.rsqrt(jnp.mean(xc * xc, axis=-1, keepdims=True) + EPS)
    return (y * g.astype(jnp.float32)).astype(x.dtype)


def modulation(cvec, w_down, w_up, b_up):
    m = (jax.nn.silu(cvec) @ w_down) @ w_up + b_up
    return m.reshape(m.shape[:-1] + (N_SUB, 3, D_MODEL))


def modulate(x, g, shift, scale):
    return rmsnorm(x, g) * (1 + scale) + shift


def swiglu(h, w13, w2):
    a, b = jnp.split(h @ w13, 2, axis=-1)
    return (jax.nn.silu(a) * b) @ w2


def ffn_sublayer(x, g, mod, w13, w2):
    h = modulate(x, g, mod[:, :, 0], mod[:, :, 1])
    return x + 0.5 * mod[:, :, 2] * swiglu(h, w13, w2)


def axial_rope(rows, dim):
    r = jnp.repeat(jnp.arange(rows, dtype=jnp.float32), GRID_W)
    col = jnp.tile(jnp.arange(GRID_W, dtype=jnp.float32), rows)
    n_freq = dim // 4
    inv = ROPE_THETA ** (-jnp.arange(n_freq, dtype=jnp.float32) / n_freq)
    ang = jnp.concatenate([r[:, None] * inv, col[:, None] * inv], axis=-1)
    return jnp.cos(ang), jnp.sin(ang)


def apply_rope(x, cos, sin):
    shp = (cos.shape[0],) + (1,) * (x.ndim - 3) + (cos.shape[1],)
    c = cos.reshape(shp)
    s = sin.reshape(shp)
    xp = x.astype(jnp.float32).reshape(x.shape[:-1] + (x.shape[-1] // 2, 2))
    x1, x2 = xp[..., 0], xp[..., 1]
    out = jnp.stack([x1 * c - x2 * s, x1 * s + x2 * c], axis=-1)
    return out.reshape(x.shape).astype(x.dtype)


def gmlp_branch(uv, norm_g, ws, bs):
    u, v = jnp.split(jax.nn.gelu(uv), 2, axis=-1)
    v = layernorm(v, norm_g)
    Bn, T, _ = v.shape
    v = v.reshape(Bn, T // CHUNK, CHUNK, A_GROUPS, A_GDIM)
    mixed = jnp.einsum('gpq,bnqgc->bnpgc', ws, v) + bs.T[:, :, None]
    return u * mixed.reshape(Bn, T, BRANCH_WIDTH)


def mlstm_scan(q, k, v, i_pre, f_pre, state):
    Bn, H, T, _ = q.shape
    nc = T // CHUNK

    def chunks(a):
        a = a.astype(jnp.float32)
        return jnp.moveaxis(a.reshape((Bn, H, nc, CHUNK) + a.shape[3:]), 2, 0)

    tri = jnp.tril(jnp.ones((CHUNK, CHUNK), dtype=bool))

    def step(carry, inp):
        C, n, m = carry
        qc, kc, vc, ic, fc = inp
        b = jnp.cumsum(jax.nn.log_sigmoid(fc), axis=-1)
        a = b + m[..., None]
        dmat = jnp.where(tri, b[..., :, None] - b[..., None, :] + ic[..., None, :], -jnp.inf)
        m_t = jnp.maximum(a, jnp.max(dmat, axis=-1))
        wa = jnp.exp(a - m_t)
        s = jnp.einsum('bhtd,bhsd->bhts', qc, kc) * jnp.exp(dmat - m_t[..., None])
        num = wa[..., None] * jnp.einsum('bhtd,bhde->bhte', qc, C) + jnp.einsum('bhts,bhse->bhte', s, vc)
        den = wa * jnp.einsum('bhtd,bhd->bht', qc, n) + jnp.sum(s, axis=-1)
        h = num / jnp.maximum(jnp.abs(den), jnp.exp(-m_t))[..., None]
        b_last = b[..., -1]
        g = b_last[..., None] - b + ic
        m_new = jnp.maximum(b_last + m, jnp.max(g, axis=-1))
        decay = jnp.exp(b_last + m - m_new)
        wg = jnp.exp(g - m_new[..., None])
        C_new = decay[..., None, None] * C + jnp.einsum('bhs,bhsd,bhse->bhde', wg, kc, vc)
        n_new = decay[..., None] * n + jnp.einsum('bhs,bhsd->bhd', wg, kc)
        return (C_new, n_new, m_new), h

    state, h = lax.scan(step, state, (chunks(q), chunks(k), chunks(v), chunks(i_pre), chunks(f_pre)))
    h = jnp.moveaxis(h, 0, 2).reshape(Bn, H, T, -1)
    return state, h.astype(v.dtype)


def diff_attention(q, k, v, lam):
    s = jnp.einsum('bqhmd,bkhmd->bhmqk', q, k).astype(jnp.float32) * (C_DH ** -0.5)
    p = jax.nn.softmax(s, axis=-1)
    w = p[:, :, 0] - lam * p[:, :, 1]
    return jnp.einsum('bhqk,bkhe->bqhe', w.astype(v.dtype), v)


def token_mixer(h_lat, h_ctx, rows, w_in, gmlp_norm_g, gmlp_ws, gmlp_bs, mlstm_gate_b, mlstm_norm_g,
                diff_lambda, diff_norm_g, w_branch, w_out, lambda_init, ctx_out):
    Bn, S, _ = h_lat.shape
    Tc = h_ctx.shape[1]
    zl = jnp.split(h_lat @ w_in, SPLIT_POINTS, axis=-1)
    zc = jnp.split(h_ctx @ w_in, SPLIT_POINTS, axis=-1)

    def mlstm_inputs(z, T):
        q = z[1].reshape(Bn, T, B_HEADS, B_DQK).transpose(0, 2, 1, 3) * (B_DQK ** -0.5)
        k = z[2].reshape(Bn, T, B_HEADS, B_DQK).transpose(0, 2, 1, 3)
        v = z[3].reshape(Bn, T, B_HEADS, B_DV).transpose(0, 2, 1, 3)
        g = (z[5].astype(jnp.float32).reshape(Bn, T, 2, 2, B_HEADS) + mlstm_gate_b).transpose(2, 3, 0, 4, 1)
        return q, k, v, g

    def flip(a):
        return jnp.flip(a, axis=2)

    qc, kc, vc, gc = mlstm_inputs(zc, Tc)
    ql, kl, vl, gl = mlstm_inputs(zl, S)
    zero = (jnp.zeros((Bn, B_HEADS, B_DQK, B_DV), jnp.float32),
            jnp.zeros((Bn, B_HEADS, B_DQK), jnp.float32),
            jnp.zeros((Bn, B_HEADS), jnp.float32))
    st_f, hc_f = mlstm_scan(qc, kc, vc, gc[0, 0], gc[0, 1], zero)
    _, hl_f = mlstm_scan(ql, kl, vl, gl[0, 0], gl[0, 1], st_f)
    st_b, hc_b = mlstm_scan(flip(qc), flip(kc), flip(vc), flip(gc[1, 0]), flip(gc[1, 1]), zero)
    _, hl_b = mlstm_scan(flip(ql), flip(kl), flip(vl), flip(gl[1, 0]), flip(gl[1, 1]), st_b)

    def mlstm_out(h, o, T):
        h = rmsnorm(h.transpose(0, 2, 1, 3), mlstm_norm_g.reshape(B_HEADS, B_DV))
        return jax.nn.sigmoid(o) * h.reshape(Bn, T, BRANCH_WIDTH)

    lam_p = diff_lambda.astype(jnp.float32)
    lam = jnp.exp(jnp.sum(lam_p[0] * lam_p[1])) - jnp.exp(jnp.sum(lam_p[2] * lam_p[3])) + lambda_init
    cos, sin = axial_rope(rows, C_DH)
    q_l = apply_rope(zl[6].reshape(Bn, S, C_HEADS, 2, C_DH), cos, sin)
    k_l = apply_rope(zl[7].reshape(Bn, S, C_HEADS, 2, C_DH), cos, sin)
    v_l = zl[8].reshape(Bn, S, C_HEADS, C_DV)
    k_c = zc[7].reshape(Bn, Tc, C_HEADS, 2, C_DH)
    v_c = zc[8].reshape(Bn, Tc, C_HEADS, C_DV)
    k_all = jnp.concatenate([k_c, k_l], axis=1)
    v_all = jnp.concatenate([v_c, v_l], axis=1)
    qb = jnp.moveaxis(q_l.reshape(Bn, S // CHUNK, CHUNK, C_HEADS, 2, C_DH), 1, 0)
    o_l = lax.map(lambda qi: diff_attention(qi, k_all, v_all, lam), qb)
    o_l = jnp.moveaxis(o_l, 0, 1).reshape(Bn, S, C_HEADS, C_DV)

    def diff_out(o, T):
        return (rmsnorm(o, diff_norm_g) * (1.0 - lambda_init)).reshape(Bn, T, BRANCH_WIDTH)

    def merge(ya, yb, yc, gate_pre):
        g = jax.nn.sigmoid(gate_pre.reshape(gate_pre.shape[:-1] + (N_BRANCH, D_MODEL)))
        y = (g[..., 0, :] * (ya @ w_branch[0]) + g[..., 1, :] * (yb @ w_branch[1])
             + g[..., 2, :] * (yc @ w_branch[2]))
        return y @ w_out

    y_lat = merge(gmlp_branch(zl[0], gmlp_norm_g, gmlp_ws, gmlp_bs),
                  mlstm_out(hl_f + flip(hl_b), zl[4], S),
                  diff_out(o_l, S), zl[9])
    if not ctx_out:
        return y_lat, None
    q_c = zc[6].reshape(Bn, Tc, C_HEADS, 2, C_DH)
    o_c = diff_attention(q_c, k_c, v_c, lam)
    y_ctx = merge(gmlp_branch(zc[0], gmlp_norm_g, gmlp_ws, gmlp_bs),
                  mlstm_out(hc_f + flip(hc_b), zc[4], Tc),
                  diff_out(o_c, Tc), zc[9])
    return y_lat, y_ctx


def setup_inputs(seed: int = 0) -> dict:
    key = jax.random.key(seed)
    ks = jax.random.split(key, 24)
    D = D_MODEL

    def nrm(k, shape, scale):
        return jax.random.normal(k, shape, jnp.float32) * scale

    i_bias = nrm(ks[14], (DEPTH, 2, 1, B_HEADS), 0.1)
    f_bias = jnp.linspace(3.0, 6.0, B_HEADS, dtype=jnp.float32) + nrm(ks[15], (DEPTH, 2, 1, B_HEADS), 0.1)
    return {
        'x': nrm(ks[0], (BATCH, SEQ, D), 1.0),
        'c': nrm(ks[1], (BATCH, D), 1.0),
        'ctx': nrm(ks[2], (BATCH, CTX_LEN, D), 1.0),
        'c_ctx': nrm(ks[3], (D,), 1.0),
        'ada_down': nrm(ks[4], (DEPTH, D, ADA_RANK), D ** -0.5),
        'ada_up': nrm(ks[5], (DEPTH, ADA_RANK, N_SUB * 3 * D), 0.3 * ADA_RANK ** -0.5),
        'ada_bias': nrm(ks[6], (DEPTH, N_SUB * 3 * D), 0.02),
        'norm_g': 1.0 + nrm(ks[7], (DEPTH, N_SUB, D), 0.02),
        'ffn_w13': nrm(ks[8], (DEPTH, 2, D, 2 * FFN_DIM), D ** -0.5),
        'ffn_w2': nrm(ks[9], (DEPTH, 2, FFN_DIM, D), FFN_DIM ** -0.5),
        'w_in': nrm(ks[10], (DEPTH, D, IN_COLS), D ** -0.5),
        'gmlp_norm_g': 1.0 + nrm(ks[11], (DEPTH, BRANCH_WIDTH), 0.02),
        'gmlp_ws': nrm(ks[12], (DEPTH, A_GROUPS, CHUNK, CHUNK), CHUNK ** -0.5),
        'gmlp_bs': 1.0 + nrm(ks[13], (DEPTH, A_GROUPS, CHUNK), 0.02),
        'mlstm_gate_b': jnp.concatenate([i_bias, f_bias], axis=2),
        'mlstm_norm_g': 1.0 + nrm(ks[16], (DEPTH, BRANCH_WIDTH), 0.02),
        'diff_lambda': nrm(ks[17], (DEPTH, 4, C_DH), 0.1),
        'diff_norm_g': 1.0 + nrm(ks[18], (DEPTH, C_DV), 0.02),
        'w_branch': nrm(ks[19], (DEPTH, N_BRANCH, BRANCH_WIDTH, D), BRANCH_WIDTH ** -0.5),
        'w_out': nrm(ks[20], (DEPTH, D, D), D ** -0.5),
        'final_g': 1.0 + nrm(ks[21], (D,), 0.02),
    }


def reference(x, c, ctx, c_ctx, ada_down, ada_up, ada_bias, norm_g, ffn_w13, ffn_w2, w_in,
              gmlp_norm_g, gmlp_ws, gmlp_bs, mlstm_gate_b, mlstm_norm_g, diff_lambda, diff_norm_g,
              w_branch, w_out, final_g):
    rows = x.shape[1] // GRID_W
    for l in range(DEPTH):
        last = l == DEPTH - 1
        lambda_init = 0.8 - 0.6 * math.exp(-0.3 * l)
        ml = modulation(c, ada_down[l], ada_up[l], ada_bias[l])[:, None]
        mc = modulation(c_ctx, ada_down[l], ada_up[l], ada_bias[l])[None, None]
        x = ffn_sublayer(x, norm_g[l, 0], ml[:, :, 0], ffn_w13[l, 0], ffn_w2[l, 0])
        ctx = ffn_sublayer(ctx, norm_g[l, 0], mc[:, :, 0], ffn_w13[l, 0], ffn_w2[l, 0])
        h_lat = modulate(x, norm_g[l, 1], ml[:, :, 1, 0], ml[:, :, 1, 1])
        h_ctx = modulate(ctx, norm_g[l, 1], mc[:, :, 1, 0], mc[:, :, 1, 1])
        y_lat, y_ctx = token_mixer(h_lat, h_ctx, rows, w_in[l], gmlp_norm_g[l], gmlp_ws[l], gmlp_bs[l],
                                   mlstm_gate_b[l], mlstm_norm_g[l], diff_lambda[l], diff_norm_g[l],
                                   w_branch[l], w_out[l], lambda_init, not last)
        x = x + ml[:, :, 1, 2] * y_lat
        x = ffn_sublayer(x, norm_g[l, 2], ml[:, :, 2], ffn_w13[l, 1], ffn_w2[l, 1])
        if not last:
            ctx = ctx + mc[:, :, 1, 2] * y_ctx
            ctx = ffn_sublayer(ctx, norm_g[l, 2], mc[:, :, 2], ffn_w13[l, 1], ffn_w2[l, 1])
    return rmsnorm(x, final_g)
```

```python
import math
import os
from contextlib import ExitStack
import numpy as np
import ml_dtypes
import concourse.bass as bass
import concourse.mybir as mybir
from concourse.bass_utils import run_bass_kernel_spmd

F32 = mybir.dt.float32
BF16 = mybir.dt.bfloat16
AF = mybir.ActivationFunctionType
ALU = mybir.AluOpType
AX = mybir.AxisListType
NCORES = 8
EPS = 1e-6


class Cfg:
    def __init__(self, D=4096, F=4096, S=8192, TC=256, L=4, R=256, mixer=True):
        self.D, self.F, self.S, self.TC, self.L, self.R = D, F, S, TC, L, R
        self.TL = S // NCORES
        self.T = TC + self.TL
        self.KC = D // 128
        self.FK = F // 128
        self.RK = R // 128
        self.NCH = self.T // 128
        self.tiles = [(o, min(512, self.T - o)) for o in range(0, self.T, 512)]
        assert len(self.tiles) <= 3
        self.mixer = mixer
        self.NMOD = 9 * self.KC


class BSP:
    def __init__(self, nc, npairs=24, cap=3000):
        self.nc = nc
        self.engs = {"sync": nc.sync, "act": nc.scalar, "dve": nc.vector, "pool": nc.gpsimd, "pe": nc.tensor}
        self.sems = []
        for i in range(npairs):
            self.sems.append((nc.alloc_semaphore(f"bS{i}"), nc.alloc_semaphore(f"bD{i}"),
                              nc.alloc_semaphore(f"bP{i}")))
        self.cc = nc.alloc_semaphore("bCC")
        self.pair = 0
        self.cap = cap
        self.tot = {}
        self.targets = {}
        self.waited = {e: {} for e in self.engs}
        self.started = set()
        self.last = {}
        self.nsteps = 0
        self.ninstr = 0

    def _semobj(self, key):
        if key == "cc":
            return self.cc
        kind, i = key
        return self.sems[i][{"S": 0, "D": 1, "P": 2}[kind]]

    def _pre(self, en):
        if en in self.started:
            return
        self.started.add(en)
        e = self.engs[en]
        for key, tot in self.targets.items():
            if self.waited[en].get(key, 0) < tot:
                e.wait_ge(self._semobj(key), tot)
                self.waited[en][key] = tot

    def _inc(self, key, v):
        self.tot[key] = self.tot.get(key, 0) + v

    def op(self, en, fn):
        self._pre(en)
        ins = fn(self.engs[en])
        self.ninstr += 1
        if en == "pe":
            self.last[en] = ins
        else:
            key = ("S", self.pair)
            ins.then_inc(self._semobj(key), 1)
            self._inc(key, 1)
        return ins

    def dma(self, en, out, in_):
        self._pre(en)
        key = ("P" if en == "pool" else "D", self.pair)
        try:
            ins = self.engs[en].dma_start(out=out, in_=in_)
        except Exception:
            print("DMA FAIL", en, "OUT", out.shape, out, "IN", in_.shape, in_)
            raise
        ins.then_inc(self._semobj(key), 16)
        self._inc(key, 16)
        self.ninstr += 1
        return ins

    def collective(self, kind, ins, outs):
        self._pre("pool")
        i = self.nc.gpsimd.collective_compute(
            kind, ALU.bypass, replica_groups=[list(range(NCORES))], ins=ins, outs=outs)
        i.then_inc(self.cc)
        self._inc("cc", 1)
        return i

    def step(self):
        if "pe" in self.last:
            key = ("S", self.pair)
            self.last["pe"].then_inc(self._semobj(key), 1)
            self._inc(key, 1)
        self.last = {}
        self.started = set()
        self.targets = dict(self.tot)
        self.nsteps += 1
        kS, kD, kP = ("S", self.pair), ("D", self.pair), ("P", self.pair)
        if (self.tot.get(kS, 0) > self.cap or self.tot.get(kD, 0) > 16 * self.cap
                or self.tot.get(kP, 0) > 16 * self.cap):
            self.pair += 1
            assert self.pair < len(self.sems), "out of semaphore pairs"

    def finish(self):
        self.step()
        for en in ("sync",):
            self._pre(en)


def sequential(B, stages, n):
    for i in range(n):
        for st in stages:
            st(i)
            B.step()


def pipeline(B, stages, n):
    ns = len(stages)
    for s in range(n + ns - 1):
        for k, st in enumerate(stages):
            i = s - k
            if 0 <= i < n and st is not None:
                st(i)
        B.step()


def weight_specs(cfg):
    D, F = cfg.D, cfg.F
    sp = {
        "adaup": (cfg.R, 9 * D),
        "w13a": (D, 2 * F), "w2a": (F, D),
        "w13b": (D, 2 * F), "w2b": (F, D),
    }
    if cfg.mixer:
        sp.update({"win": (D, 8192 + 3 * D), "wbr0": (1024, D), "wbr1": (1024, D), "wbr2": (1024, D),
                   "wout": (D, D)})
    return sp


class Prog:
    def __init__(self, cfg):
        self.cfg = cfg
        self.nc = bass.Bass("TRN2", target_bir_lowering=False)
        self.B = BSP(self.nc, cap=int(os.environ.get('BSP_CAP', 3000)))
        self.inputs = {}
        self.dram = {}

    def din(self, name, shape, dt=F32):
        t = self.nc.dram_tensor(name, list(shape), dt, kind="ExternalInput")
        self.inputs[name] = (tuple(shape), dt)
        return t

    def dscratch(self, name, shape, dt):
        return self.nc.dram_tensor(name, list(shape), dt)


def build(cfg):
    P = Prog(cfg)
    nc, B = P.nc, P.B
    D, F, T, KC, FK, L = cfg.D, cfg.F, cfg.T, cfg.KC, cfg.FK, cfg.L
    TC, TL = cfg.TC, cfg.TL
    tiles = cfg.tiles
    specs = weight_specs(cfg)

    xT0 = P.din("xT0", [D, T])
    cT = P.din("cT", [128, KC, 2])
    consts = P.din("consts", [128, 4, 128])
    normg = P.din("normg", [128, L, 3, KC])
    finalg = P.din("finalg", [128, KC])
    adab = P.din("adab", [128, L, cfg.NMOD])
    adown = P.din("adown", [L * D, cfg.R])
    win_sh = {}
    for l in range(L):
        for nm, (K, N) in specs.items():
            win_sh[(l, nm)] = P.din(f"{nm}_{l}", [K, N // NCORES])
    outT = nc.dram_tensor("outT", [D, TL], F32, kind="ExternalOutput")
    dbg = nc.dram_tensor("dbg", [3 * 1024, T], BF16, kind="ExternalOutput") if getattr(cfg, "debug", False) else None

    xT = P.dscratch("xT", [D, T], F32)
    wloc, wful = {}, {}
    for nm, (K, N) in specs.items():
        nbl = N // NCORES // 128
        for par in range(L):
            wloc[(par, nm)] = P.dscratch(f"wl_{nm}_{par}", [nbl * 128, K], BF16)
            wful[(par, nm)] = P.dscratch(f"wf_{nm}_{par}", [NCORES * nbl * 128, K], BF16)
    adown_b = P.dscratch("adown_b", [cfg.RK * 128, D], BF16)
    if cfg.mixer:
        NCH = cfg.NCH
        NT = TC + cfg.S
        NCHG = NT // 128
        WS = 4 * T + NCH * 520
        snd_l = [P.dscratch(f"snd{i}", [8 * 128, WS], BF16) for i in range(L)]
        rcv_l = [P.dscratch(f"rcv{i}", [64 * 128, WS], BF16) for i in range(L)]
        ret_l = [P.dscratch(f"ret{i}", [8 * T, 512], BF16) for i in range(L)]
        rrcv_l = [P.dscratch(f"rrcv{i}", [64 * T, 512], BF16) for i in range(L)]
        mysel = P.dscratch("mysel", [8 * 128, WS], BF16)
        myret = P.dscratch("myret", [8 * T, 512], BF16)
        uTd = P.dscratch("uTd", [1024, T], BF16)
        vgd = P.dscratch("vgd", [T, 1024], F32)
        ogd = P.dscratch("ogd", [T, 1024], BF16)
        gTd = P.dscratch("gTd", [3 * D, T], BF16)
        wg16 = P.din("wg16", [128, L, KC, 16])
        gateb = P.din("gateb", [128, L, 16])
        gng = P.din("gng", [128, L, 1024])
        mng = P.din("mng", [128, L, 1024])
        dng = P.din("dng", [128, L, 1024])
        gbs = P.din("gbs", [128, L, 1024])
        gws = P.din("gws", [128, L, 8, 128])
        dlam = P.din("dlam", [128, L, 4, 128])
        ropet = P.din("ropet", [128, 2, TL])
        masks = P.din("masks", [128, 4, 128])

    es = ExitStack()
    with es:
        _cnt = [0]

        def sbt(name, shape, dt):
            _cnt[0] += 1
            return nc.sbuf_tensor(f"{name}_{_cnt[0]}", list(shape), dt)

        def sb(name, shape, dt):
            return es.enter_context(sbt(name, shape, dt))

        cst = sb("cst", [128, 4, 128], F32)
        cstb = sb("cstb", [128, 4, 128], BF16)
        mod = sb("mod", [128, 2, cfg.NMOD], F32)
        ng = sb("ng", [128, L, 3, KC], F32)
        fg = sb("fg", [128, KC], F32)
        scb = sb("scb", [128, KC, 2], BF16)
        Amod = sb("Amod", [128, 2, KC], F32)
        Gmod = sb("Gmod", [128, 2, KC], F32)
        ps = es.enter_context(nc.psum_tensor("ps", [128, 4096], F32))
        epsc = sb("epsc", [128, 1], F32)
        B.op("dve", lambda e: e.memset(epsc[:], EPS))

        def bank(i, w=512, o=0):
            return ps[:, i * 512 + o: i * 512 + o + w]

        B.dma("sync", cst[:], consts[:])
        B.dma("sync", ng[:], normg[:])
        B.dma("sync", fg[:], finalg[:])
        B.step()
        B.op("dve", lambda e: e.tensor_copy(out=cstb[:], in_=cst[:]))
        B.step()
        ones_b = cstb[:, 0, :]

        def prep_matrix(src, dst, K, Nl, row0=0):
            kcw = K // 128
            nbl = Nl // 128
            G = 2 if nbl % 2 == 0 else 1
            ngrp = nbl // G
            with sbt("stg", [128, 2, kcw, G * 128], F32) as stg, \
                    sbt("wbf", [128, 2, G, kcw, 128], BF16) as wbf:
                srcv = src[row0:row0 + K, :].rearrange("(kc p) n -> p kc n", p=128)

                def s_load(i):
                    B.dma("sync", stg[:, i % 2], srcv[:, :, i * G * 128:(i + 1) * G * 128])

                def s_cast(i):
                    for g in range(G):
                        en = ["dve", "act", "pool"][(i * G + g) % 3]
                        o = wbf[:, i % 2, g]
                        a = stg[:, i % 2, :, g * 128:(g + 1) * 128]
                        if en == "act":
                            B.op(en, lambda e, o=o, a=a: e.copy(out=o, in_=a))
                        else:
                            B.op(en, lambda e, o=o, a=a: e.tensor_copy(out=o, in_=a))

                def s_store(i):
                    for g in range(G):
                        j = i * G + g
                        B.dma("pool", dst[j * 128:(j + 1) * 128, :].rearrange("p (kc c) -> p kc c", c=128),
                              wbf[:, i % 2, g])

                pipeline(B, [s_load, s_cast, s_store], ngrp)
                B.step()

        CC_MAX = 4 * 1024 * 1024

        def wchunks(nm):
            K, N = specs[nm]
            nbl = N // NCORES // 128
            bpc = max(1, CC_MAX // (128 * K * 2))
            return nbl, [(b0, min(nbl, b0 + bpc)) for b0 in range(0, nbl, bpc)]

        def prep_layer(l):
            par = l
            for nm, (K, N) in specs.items():
                prep_matrix(win_sh[(l, nm)], wloc[(par, nm)], K, N // NCORES)
            B.step()
            for nm in specs:
                nbl, chs = wchunks(nm)
                for (b0, b1) in chs:
                    B.collective("AllGather", [wloc[(par, nm)][b0 * 128:b1 * 128, :].opt()],
                                 [wful[(par, nm)][8 * b0 * 128:8 * b1 * 128, :].opt()])
            B.step()
            prep_matrix(adown, adown_b, D, cfg.R, row0=l * D)

        def wblock(l, nm, j):
            nbl, chs = wchunks(nm)
            r, jl = divmod(j, nbl)
            for (b0, b1) in chs:
                if b0 <= jl < b1:
                    row = 8 * b0 * 128 + (r * (b1 - b0) + (jl - b0)) * 128
                    return wful[(l, nm)][row:row + 128, :]
            raise AssertionError

        def modulation(l):
            RK = cfg.RK
            with sbt("adb", [128, RK, D], BF16) as adb, \
                    sbt("tb", [128, RK, 2], BF16) as tb, \
                    sbt("aub", [128, 2, 72, cfg.R], BF16) as aub, \
                    sbt("bia", [128, cfg.NMOD], F32) as bia:
                for jb in range(RK):
                    B.dma("sync", adb[:, jb], adown_b[jb * 128:(jb + 1) * 128, :])
                B.dma("sync", bia[:], adab[:, l, :])
                B.step()
                for jb in range(RK):
                    for kc in range(KC):
                        B.op("pe", lambda e, jb=jb, kc=kc: e.matmul(
                            bank(0, 2, 2 * jb), lhsT=adb[:, jb, kc * 128:(kc + 1) * 128], rhs=scb[:, kc, :],
                            start=(kc == 0), stop=(kc == KC - 1)))
                B.step()
                B.op("dve", lambda e: e.tensor_copy(out=tb[:].rearrange("p a b -> p (a b)"), in_=bank(0, 2 * RK)))
                B.step()
                nmod = cfg.NMOD
                ngr = (nmod + 71) // 72
                auf = wful[(l, "adaup")]

                def s_load(i):
                    n = min(72, nmod - i * 72)
                    nbl_, chs_ = wchunks("adaup")
                    assert len(chs_) == 1
                    B.dma("sync", aub[:, i % 2, 0:n, :],
                          auf[i * 72 * 128:(i * 72 + n) * 128, :].rearrange("(j p) c -> p j c", p=128))

                def s_mm(i):
                    n = min(72, nmod - i * 72)
                    for jj in range(n):
                        j = i * 72 + jj
                        for kc in range(RK):
                            B.op("pe", lambda e, jj=jj, j=j, kc=kc: e.matmul(
                                ps[:, 2 * j:2 * j + 2], lhsT=aub[:, i % 2, jj, kc * 128:(kc + 1) * 128],
                                rhs=tb[:, kc, :], start=(kc == 0), stop=(kc == RK - 1)))

                pipeline(B, [s_load, s_mm], ngr)
                for v in range(2):
                    B.op("dve", lambda e, v=v: e.tensor_tensor(
                        out=mod[:, v, :], in0=ps[:, 0:2 * nmod].rearrange("p (j v) -> p v j", v=2)[:, v, :],
                        in1=bia[:], op=ALU.add))
                B.step()

        def mslice(v, s, k):
            o = (s * 3 + k) * KC
            return mod[:, v, o:o + KC]

        def sub_mod(l, s, gscale):
            for v in range(2):
                B.op("dve", lambda e, v=v: e.scalar_tensor_tensor(
                    out=Amod[:, v, :], in0=mslice(v, s, 1), scalar=1.0, in1=ng[:, l, s, :],
                    op0=ALU.add, op1=ALU.mult))
                B.op("pool", lambda e, v=v: e.tensor_scalar(
                    out=Gmod[:, v, :], in0=mslice(v, s, 2), scalar1=float(gscale), scalar2=None, op0=ALU.mult))
            B.step()

        vcols = [(0, 0, TC), (1, TC, T)]

        def norm_mod(src, out_fn, extra=None):
            with sbt("xs", [128, 3, T], F32) as xs, \
                    sbt("sq", [128, 2, T], BF16) as sq, \
                    sbt("rstd", [128, T], F32) as rstd, \
                    sbt("xn", [128, 2, T], F32) as xn:
                srcv = src.ap().rearrange("(kc p) t -> p kc t", p=128)

                def s_load(kc):
                    B.dma("sync", xs[:, kc % 3], srcv[:, kc, :])

                def s_sq(kc):
                    B.op("act", lambda e: e.activation(out=sq[:, kc % 2], in_=xs[:, kc % 3], func=AF.Square))

                def s_mm(kc):
                    for ti, (o, w) in enumerate(tiles):
                        B.op("pe", lambda e, ti=ti, o=o, w=w: e.matmul(
                            bank(ti, w), lhsT=ones_b, rhs=sq[:, kc % 2, o:o + w],
                            start=(kc == 0), stop=(kc == KC - 1)))

                pipeline(B, [s_load, s_sq, s_mm], KC)
                for ti, (o, w) in enumerate(tiles):
                    B.op("act", lambda e, ti=ti, o=o, w=w: e.activation(
                        out=rstd[:, o:o + w], in_=bank(ti, w), func=AF.Sqrt, scale=1.0 / D, bias=epsc[:, 0:1]))
                B.step()
                B.op("dve", lambda e: e.reciprocal(out=rstd[:], in_=rstd[:]))
                B.step()

                def s_mul(kc):
                    en = "dve" if kc % 2 == 0 else "pool"
                    B.op(en, lambda e: e.tensor_tensor(out=xn[:, kc % 2], in0=xs[:, kc % 3], in1=rstd[:],
                                                       op=ALU.mult))

                def s_out(kc):
                    out_fn(kc, xn[:, kc % 2])

                st = [s_load, s_mul, s_out] + ([extra] if extra else [])
                pipeline(B, st, KC)
                B.step()

        def hT_out(hT, s):
            def f(kc, xa):
                for v, c0, c1 in vcols:
                    B.op("act", lambda e, v=v, c0=c0, c1=c1: e.activation(
                        out=hT[:, kc, c0:c1], in_=xa[:, c0:c1], func=AF.Identity,
                        bias=mslice(v, s, 0)[:, kc:kc + 1], scale=Amod[:, v, kc:kc + 1]))
            return f

        def mm_block(wt, rhsT, kcn, pset):
            for kc in range(kcn):
                for ti, (o, w) in enumerate(tiles):
                    B.op("pe", lambda e, kc=kc, ti=ti, o=o, w=w: e.matmul(
                        bank(3 * pset + ti, w), lhsT=wt[:, kc * 128:(kc + 1) * 128], rhs=rhsT[:, kc, o:o + w],
                        start=(kc == 0), stop=(kc == kcn - 1)))

        def pview(pset):
            return ps[:, 3 * pset * 512: 3 * pset * 512 + T] if len(tiles) > 1 or True else None

        def evac_tiles(pset):
            return [(bank(3 * pset + ti, w), o, w) for ti, (o, w) in enumerate(tiles)]

        def gemm_residual(l, nm, rhsT, kcn):
            nb = KC
            with sbt("wr", [128, 3, kcn * 128], BF16) as wr, \
                    sbt("xb", [128, 3, T], F32) as xb, \
                    sbt("xo", [128, 2, T], F32) as xo:
                xv = xT.ap().rearrange("(kc p) t -> p kc t", p=128)

                def s_load(j):
                    B.dma("sync", wr[:, j % 3], wblock(l, nm, j))
                    B.dma("sync", xb[:, j % 3], xv[:, j, :])

                def s_mm(j):
                    mm_block(wr[:, j % 3], rhsT, kcn, j % 2)

                def s_ev(j):
                    for (pa, o, w) in evac_tiles(j % 2):
                        for v, c0, c1 in vcols:
                            a, b = max(o, c0), min(o + w, c1)
                            if a >= b:
                                continue
                            B.op("dve", lambda e, pa=pa, o=o, a=a, b=b, v=v: e.scalar_tensor_tensor(
                                out=xo[:, j % 2, a:b], in0=pa[:, a - o:b - o], scalar=Gmod[:, v, j:j + 1],
                                in1=xb[:, j % 3, a:b], op0=ALU.mult, op1=ALU.add))

                def s_st(j):
                    B.dma("pool", xv[:, j, :], xo[:, j % 2])

                pipeline(B, [s_load, s_mm, s_ev, s_st], nb)
                B.step()

        def ffn(l, s, w13, w2):
            sub_mod(l, s, 0.5)
            with sbt("hid", [128, FK, T], BF16) as hid:
                with sbt("hT", [128, KC, T], BF16) as hT:
                    norm_mod(xT, hT_out(hT, s))
                    with sbt("wa", [128, 2, D], BF16) as wa, \
                            sbt("wb", [128, 2, D], BF16) as wb, \
                            sbt("sa", [128, 2, T], BF16) as sa:
                        B.dma("sync", wa[:, 0], wblock(l, w13, 0))
                        B.dma("sync", wb[:, 0], wblock(l, w13, FK + 0))
                        B.step()
                        for j in range(FK + 1):
                            if j < FK:
                                mm_block(wa[:, j % 2], hT, KC, 0)
                                if j + 1 < FK:
                                    B.dma("sync", wa[:, (j + 1) % 2], wblock(l, w13, j + 1))
                            if j > 0:
                                for (pa, o, w) in evac_tiles(1):
                                    B.op("dve", lambda e, pa=pa, o=o, w=w: e.tensor_tensor(
                                        out=hid[:, j - 1, o:o + w], in0=pa, in1=sa[:, (j - 1) % 2, o:o + w],
                                        op=ALU.mult))
                            B.step()
                            if j < FK:
                                mm_block(wb[:, j % 2], hT, KC, 1)
                                if j + 1 < FK:
                                    B.dma("sync", wb[:, (j + 1) % 2], wblock(l, w13, FK + j + 1))
                                for (pa, o, w) in evac_tiles(0):
                                    B.op("act", lambda e, pa=pa, o=o, w=w: e.activation(
                                        out=sa[:, j % 2, o:o + w], in_=pa, func=AF.Silu))
                                B.step()
                gemm_residual(l, w2, hid, FK)

        def final_norm():
            with sbt("ho", [128, 2, TL], F32) as ho:
                ov = outT.ap().rearrange("(kc p) t -> p kc t", p=128)

                def f(kc, xa):
                    B.op("act", lambda e: e.activation(out=ho[:, kc % 2, :], in_=xa[:, TC:T],
                                                       func=AF.Identity, scale=fg[:, kc:kc + 1]))

                def s_st(kc):
                    B.dma("pool", ov[:, kc, :], ho[:, kc % 2])

                norm_mod(xT, f, extra=s_st)

        if cfg.mixer:
            NCc = TC // 128
            NLC = TL // 128
            class _V:
                pass
            V = _V()

            def bind_layer(l):
                V.snd, V.rcv, V.ret, V.rrcv = snd_l[l], rcv_l[l], ret_l[l], rrcv_l[l]
                V.sndv = V.snd.ap().rearrange("(j p) w -> p j w", p=128)
                V.sndv2 = V.snd.ap().rearrange("(h two p) w -> p two h w", two=2, p=128)
                V.retv = V.ret.ap().rearrange("(r t) c -> r t c", r=8)
            ident_b = cstb[:, 1, :]
            perm_b = cstb[:, 3, :]

        def cbase(c):
            return 4 * T + c * 520

        def fm_proj(l, hT, blocks, epi):
            with sbt("wrp", [128, 2, D], BF16) as wr:
                def s_load(i):
                    B.dma("sync", wr[:, i % 2], wblock(l, "win", blocks[i]))

                def s_mm(i):
                    mm_block(wr[:, i % 2], hT, KC, i % 2)

                st = [s_load, s_mm] + [(lambda i, f=f: f(i, blocks[i], i % 2)) for f in epi]
                pipeline(B, st, len(blocks))
                B.step()

        def epi_simple(ob, func, scale, dests):
            def e1(i, j, pset):
                for ti, (o, w) in enumerate(tiles):
                    B.op("act", lambda e, ti=ti, o=o, w=w: e.activation(
                        out=ob[:, i % 3, o:o + w], in_=bank(3 * pset + ti, w), func=func, scale=scale))

            def e2(i, j, pset):
                for d in dests(j):
                    B.dma("pool", d, ob[:, i % 3])
            return [e1, e2]

        def gelu_stages(xa, xb, srcs, out_fn):
            RG = 8

            def g1(i, j, pset):
                for (pa, o, w) in srcs(i, pset):
                    B.op("dve", lambda e, pa=pa, o=o, w=w: e.tensor_copy(out=xa[:, i % RG, o:o + w], in_=pa))

            def g1b(i, j, pset):
                B.op("act", lambda e: e.activation(out=xb[:, i % RG], in_=xa[:, i % RG], func=AF.Square))

            def g2(i, j, pset):
                B.op("dve", lambda e: e.tensor_scalar(out=xb[:, i % RG], in0=xb[:, i % RG], scalar1=0.044715,
                                                      scalar2=1.0, op0=ALU.mult, op1=ALU.add))

            def g3(i, j, pset):
                B.op("dve", lambda e: e.tensor_tensor(out=xb[:, i % RG], in0=xb[:, i % RG], in1=xa[:, i % RG],
                                                      op=ALU.mult))

            def g4(i, j, pset):
                B.op("act", lambda e: e.activation(out=xb[:, i % RG], in_=xb[:, i % RG], func=AF.Sigmoid,
                                                   scale=1.5957691216))

            def g5(i, j, pset):
                B.op("dve", lambda e: e.tensor_tensor(out=out_fn(i), in0=xb[:, i % RG], in1=xa[:, i % RG],
                                                      op=ALU.mult))
            return [g1, g1b, g2, g3, g4, g5]

        def tm_proj(l, hT, blocks, epi):
            groups = [blocks[k:k + 4] for k in range(0, len(blocks), 4)]
            items = [(g, c) for g in range(len(groups)) for c in range(NCH)]
            with sbt("wtm", [128, 2, 4, D], BF16) as wt:
                def s_load(i):
                    g, c = items[i]
                    if c == 0:
                        for jj, j in enumerate(groups[g]):
                            B.dma("sync", wt[:, g % 2, jj], wblock(l, "win", j))

                def s_mm(i):
                    g, c = items[i]
                    nb = len(groups[g])
                    for kc in range(KC):
                        for jj in range(nb):
                            B.op("pe", lambda e, kc=kc, jj=jj: e.matmul(
                                bank(6 + i % 2, 128, jj * 128), lhsT=hT[:, kc, c * 128:(c + 1) * 128],
                                rhs=wt[:, g % 2, jj, kc * 128:(kc + 1) * 128],
                                start=(kc == 0 and jj == 0), stop=(kc == KC - 1), skip_group_check=True))

                st = [s_load, s_mm] + [(lambda i, f=f: f(i, items[i], 6 + i % 2)) for f in epi]
                pipeline(B, st, len(items))
                B.step()

        def tm_simple(ot, func, dests):
            def e1(i, gc, bk):
                B.op("act", lambda e: e.activation(out=ot[:, i % 3], in_=bank(bk, 512), func=func))

            def e2(i, gc, bk):
                for (d, s_) in dests(gc, ot[:, i % 3]):
                    B.dma("pool", d, s_)
            return [e1, e2]

        def proj_phase(l, hT, stop=None):
            if stop == "p_u_simple":
                with sbt("obx", [128, 3, T], BF16) as ob:
                    fm_proj(l, hT, list(range(0, 8)), epi_simple(
                        ob, AF.Identity, 1.0, lambda j: [uTd[j * 128:(j + 1) * 128, :]]))
                return
            if stop and stop.startswith("p_u_g"):
                ng = int(stop[5:])
                with sbt("gxa", [128, 8, T], F32) as xa, sbt("gxb", [128, 8, T], F32) as xb, \
                        sbt("gob", [128, 3, T], BF16) as gob:
                    def srcs(i, pset):
                        return evac_tiles(pset)
                    fm_proj(l, hT, list(range(0, 8)), gelu_stages(xa, xb, srcs, lambda i: gob[:, i % 3])[:ng])
                return
            with sbt("gxa", [128, 8, T], F32) as xa, sbt("gxb", [128, 8, T], F32) as xb, \
                    sbt("gob", [128, 3, T], BF16) as gob:
                def srcs(i, pset):
                    return evac_tiles(pset)

                def g6(i, j, pset):
                    B.dma("pool", uTd[j * 128:(j + 1) * 128, :], gob[:, i % 3])
                fm_proj(l, hT, list(range(0, 8)), gelu_stages(xa, xb, srcs, lambda i: gob[:, i % 3]) + [g6])
            if stop == "p_u":
                return
            with sbt("gxa2", [128, 8, 512], F32) as xa, sbt("gxb2", [128, 8, 512], F32) as xb, \
                    sbt("gob2", [128, 3, 512], F32) as gob:
                def srcs2(i, bk):
                    return [(bank(bk, 512), 0, 512)]

                def g6b(i, gc, bk):
                    g, c = gc
                    B.dma("pool", vgd[c * 128:(c + 1) * 128, g * 512:(g + 1) * 512], gob[:, i % 3])
                tm_proj(l, hT, list(range(8, 16)), gelu_stages(xa, xb, srcs2, lambda i: gob[:, i % 3]) + [g6b])
            if stop == "p_v":
                return
            with sbt("ob", [128, 3, T], BF16) as ob:
                fm_proj(l, hT, list(range(16, 20)), epi_simple(
                    ob, AF.Identity, 128 ** -0.5, lambda j: [V.sndv2[:, hf, j - 16, 0:T] for hf in range(2)]))
                fm_proj(l, hT, list(range(20, 24)), epi_simple(
                    ob, AF.Identity, 1.0, lambda j: [V.sndv2[:, hf, j - 20, T:2 * T] for hf in range(2)]))
                fm_proj(l, hT, list(range(64, 64 + 3 * KC)), epi_simple(
                    ob, AF.Sigmoid, 1.0, lambda j: [gTd[(j - 64) * 128:(j - 63) * 128, :]]))
            if stop == "p_fm":
                return
            with sbt("ot", [128, 3, 512], BF16) as ot:
                tm_proj(l, hT, list(range(20, 24)), tm_simple(ot, AF.Identity, lambda gc, s_: [
                    (V.sndv2[:, hf, :, cbase(gc[1]):cbase(gc[1]) + 128], s_.rearrange("p (h c) -> p h c", h=4))
                    for hf in range(2)]))
                tm_proj(l, hT, list(range(24, 32)), tm_simple(ot, AF.Identity, lambda gc, s_: [
                    (V.sndv[:, 4 * gc[0]:4 * gc[0] + 4, cbase(gc[1]) + 128:cbase(gc[1]) + 256],
                     s_.rearrange("p (h c) -> p h c", h=4))]))
                tm_proj(l, hT, list(range(32, 40)), tm_simple(ot, AF.Sigmoid, lambda gc, s_: [
                    (ogd[gc[1] * 128:(gc[1] + 1) * 128, gc[0] * 512:(gc[0] + 1) * 512], s_)]))
                tm_proj(l, hT, list(range(56, 64)), tm_simple(ot, AF.Identity, lambda gc, s_: [
                    (V.sndv2[:, hf, 2 * gc[0]:2 * gc[0] + 2, cbase(gc[1]) + 256:cbase(gc[1]) + 512],
                     s_.rearrange("p (h c) -> p h c", h=2)) for hf in range(2)]))
            if stop == "p_tm":
                return
            gates_proj(l, hT)
            if stop == "p_gates":
                return
            rope_proj(l, hT)

        def gates_proj(l, hT):
            Rg = 10
            with sbt("wg32", [128, KC, 16], F32) as wg32, sbt("wgb", [128, KC, 16], BF16) as wgb, \
                    sbt("gb", [128, 16], F32) as gb, sbt("g32", [128, Rg, 16], F32) as g32, \
                    sbt("lfb", [128, Rg, 2, 4], F32) as lfb, sbt("gsl", [128, Rg, 4, 4], F32) as gsl, \
                    sbt("ghl", [128, Rg, 4, 8], BF16) as ghl:
                B.dma("sync", wg32[:], wg16[:, l])
                B.dma("sync", gb[:], gateb[:, l])
                B.step()
                B.op("dve", lambda e: e.tensor_copy(out=wgb[:], in_=wg32[:]))
                B.step()

                def pg(c):
                    return ps[:, (6 + c % 2) * 512:(6 + c % 2) * 512 + 16]

                def s_mm(c):
                    for kc in range(KC):
                        B.op("pe", lambda e, kc=kc: e.matmul(pg(c), lhsT=hT[:, kc, c * 128:(c + 1) * 128],
                                                            rhs=wgb[:, kc, :], start=(kc == 0), stop=(kc == KC - 1)))

                def s1(c):
                    B.op("dve", lambda e: e.tensor_tensor(out=g32[:, c % Rg], in0=pg(c), in1=gb[:], op=ALU.add))

                def s2(c):
                    gv = g32[:, c % Rg].rearrange("p (d k h) -> p d k h", d=2, k=2)
                    B.op("act", lambda e: e.activation(out=lfb[:, c % Rg], in_=gv[:, :, 1, :], func=AF.Sigmoid))

                def s3(c):
                    B.op("act", lambda e: e.activation(out=lfb[:, c % Rg], in_=lfb[:, c % Rg], func=AF.Ln))

                def s4(c):
                    gv = g32[:, c % Rg].rearrange("p (d k h) -> p d k h", d=2, k=2)
                    for d in range(2):
                        B.op("dve", lambda e, d=d: e.tensor_copy(out=gsl[:, c % Rg, :, 2 * d], in_=gv[:, d, 0, :]))
                        B.op("pool", lambda e, d=d: e.tensor_copy(out=gsl[:, c % Rg, :, 2 * d + 1],
                                                                   in_=lfb[:, c % Rg, d, :]))

                def s5(c):
                    B.op("dve", lambda e: e.tensor_copy(out=ghl[:, c % Rg, :, 0:4], in_=gsl[:, c % Rg]))

                def s6(c):
                    B.op("dve", lambda e: e.tensor_tensor(out=ghl[:, c % Rg, :, 4:8], in0=gsl[:, c % Rg],
                                                          in1=ghl[:, c % Rg, :, 0:4], op=ALU.subtract))

                def s7(c):
                    for hf in range(2):
                        B.dma("pool", V.sndv2[:, hf, :, cbase(c) + 512:cbase(c) + 520], ghl[:, c % Rg])

                sequential(B, [s_mm, s1, s2, s3, s4, s5, s6, s7], NCH)
                B.step()

        def rope_proj(l, hT):
            with sbt("rw", [128, 2, D], BF16) as rw, sbt("rtab", [128, 2, TL], F32) as rtab, \
                    sbt("rx32", [128, T], F32) as x32, sbt("rxh", [128, T], BF16) as xh, \
                    sbt("rxl", [128, T], BF16) as xl, sbt("rt1", [128, TL], F32) as t1, \
                    sbt("rt2", [128, TL], F32) as t2, sbt("rob", [128, 2, T], BF16) as rob:
                B.dma("sync", rtab[:], ropet[:])
                blocks = list(range(40, 56))
                B.dma("sync", rw[:, 0], wblock(l, "win", blocks[0]))
                B.step()
                ltiles = [(o, min(512, TL - o)) for o in range(0, TL, 512)]
                for i, j in enumerate(blocks):
                    mm_block(rw[:, i % 2], hT, KC, 0)
                    if i + 1 < len(blocks):
                        B.dma("sync", rw[:, (i + 1) % 2], wblock(l, "win", blocks[i + 1]))
                    B.step()
                    for (pa, o, w) in evac_tiles(0):
                        B.op("act", lambda e, pa=pa, o=o, w=w: e.activation(out=x32[:, o:o + w], in_=pa,
                                                                          func=AF.Identity))
                    B.step()
                    B.op("dve", lambda e: e.tensor_copy(out=xh[:], in_=x32[:]))
                    B.step()
                    B.op("dve", lambda e: e.tensor_tensor(out=xl[:], in0=x32[:], in1=xh[:], op=ALU.subtract))
                    B.op("act", lambda e: e.activation(out=rob[:, i % 2, 0:TC], in_=x32[:, 0:TC], func=AF.Identity))
                    B.step()
                    for ti, (o, w) in enumerate(ltiles):
                        B.op("pe", lambda e, ti=ti, o=o, w=w: e.matmul(
                            bank(6 + ti, w), lhsT=perm_b, rhs=xh[:, TC + o:TC + o + w], start=True, stop=False))
                        B.op("pe", lambda e, ti=ti, o=o, w=w: e.matmul(
                            bank(6 + ti, w), lhsT=perm_b, rhs=xl[:, TC + o:TC + o + w], start=False, stop=True))
                    B.op("dve", lambda e: e.tensor_tensor(out=t1[:], in0=x32[:, TC:T], in1=rtab[:, 0, :], op=ALU.mult))
                    B.step()
                    for ti, (o, w) in enumerate(ltiles):
                        B.op("dve", lambda e, ti=ti, o=o, w=w: e.tensor_tensor(
                            out=t2[:, o:o + w], in0=bank(6 + ti, w), in1=rtab[:, 1, o:o + w], op=ALU.mult))
                    B.step()
                    B.op("dve", lambda e: e.tensor_tensor(out=rob[:, i % 2, TC:T], in0=t1[:], in1=t2[:], op=ALU.add))
                    B.step()
                    if j < 48:
                        B.dma("pool", V.sndv[:, j - 40, 2 * T:3 * T], rob[:, i % 2])
                    else:
                        B.dma("pool", V.sndv[:, j - 48, 3 * T:4 * T], rob[:, i % 2])
                B.step()

        def rcv_copy():
            pid = nc.gpsimd.partition_id()
            rv = V.rcv.ap().rearrange("(j r p) w -> j r p w", r=8, j=8, p=128)
            B.dma("pool", mysel.ap().rearrange("(r p) w -> r p w", p=128),
                  rv[bass.ds(pid, 1)].rearrange("o r p w -> (o r) p w"))
            B.step()

        def ret_copy():
            pid = nc.gpsimd.partition_id()
            rrv = V.rrcv.ap().rearrange("(r j t) c -> r j t c", j=8, r=8)
            B.dma("pool", myret.ap().rearrange("(j t) c -> j t c", j=8),
                  rrv[bass.ds(pid, 1)].rearrange("o j t c -> (o j) t c"))
            B.step()

        def rcv_sel():
            return mysel.ap().rearrange("(r p) w -> p r w", p=128)

        def load_fm(sel, dst, col0):
            B.dma("sync", dst[:, 0:TC], sel[:, 0, col0:col0 + TC])
            B.dma("sync", dst[:, TC:NT].rearrange("p (r t) -> p r t", r=8), sel[:, :, col0 + TC:col0 + T])

        def load_tm(sel, dst, off, n):
            tail = sel[:, :, 4 * T:WS].rearrange("p r (c k) -> p r c k", k=520)
            B.dma("sync", dst[:, 0:NCc, 0:n], tail[:, 0, 0:NCc, off:off + n])
            for r in range(8):
                B.dma("sync", dst[:, NCc + r * NLC:NCc + (r + 1) * NLC, 0:n], tail[:, r, NCc:NCH, off:off + n])

        def ret_dests(c, col0, n):
            if c < NCc:
                return [V.retv[r, c * 128:(c + 1) * 128, col0:col0 + n] for r in range(8)]
            r, lc = divmod(c - NCc, NLC)
            return [V.retv[r, TC + lc * 128:TC + (lc + 1) * 128, col0:col0 + n]]

        def mlstm_phase():
            sel = rcv_sel()
            NI = 2 * NCHG
            with sbt("mq", [128, NT], BF16) as qT, sbt("mkt", [128, NCHG, 128], BF16) as ktok, \
                    sbt("mSM", [128, NI, 128], BF16) as SM, sbt("mvw", [128, NI, 129], BF16) as vw, \
                    sbt("mvwL", [128, NI, 129], BF16) as vwL, sbt("meb", [128, NI], F32) as eb, \
                    sbt("meL", [128, NI], F32) as eL, sbt("mmk", [128, 4, 128], F32) as mk32, \
                    sbt("mmkb", [128, 4, 128], BF16) as mkb:
                B.dma("sync", mk32[:], masks[:])
                load_fm(sel, qT, 0)
                load_tm(sel, ktok, 0, 128)
                B.step()
                B.op("dve", lambda e: e.tensor_copy(out=mkb[:], in_=mk32[:]))
                B.step()
                with sbt("mk", [128, NT], BF16) as kT, sbt("mvx", [128, NCHG, 129], BF16) as vx, \
                        sbt("mghl", [128, NCHG, 16], BF16) as ghl, sbt("mg32", [128, NCHG, 4], F32) as g32, \
                        sbt("mw", [128, NI, 2], F32) as wv:
                    load_fm(sel, kT, T)
                    load_tm(sel, vx, 128, 128)
                    load_tm(sel, ghl, 512, 8)
                    B.op("pool", lambda e: e.memset(vx[:, :, 128:129], 1.0))
                    B.step()
                    B.op("dve", lambda e: e.tensor_tensor(out=g32[:], in0=ghl[:, :, 0:4], in1=ghl[:, :, 4:8],
                                                          op=ALU.add))
                    B.step()

                    def bc(it, k):
                        o = (4 + it % 2) * 512 + k
                        return ps[:, o:o + 1]

                    def stp(it):
                        o = (6 + it % 2) * 512
                        return ps[:, o:o + 128]

                    def p1(it):
                        d, c = divmod(it, NCHG)
                        for k, lh in enumerate([mkb[:, d, :], ones_b, mkb[:, 2 + d, :]]):
                            for hl in range(2):
                                col = hl * 4 + 2 * d + 1
                                B.op("pe", lambda e, k=k, lh=lh, hl=hl, col=col: e.matmul(
                                    bc(it, k), lhsT=lh, rhs=ghl[:, c, col:col + 1], start=(hl == 0), stop=(hl == 1)))
                        B.op("pe", lambda e: e.matmul(stp(it), lhsT=kT[:, c * 128:(c + 1) * 128],
                                                      rhs=qT[:, c * 128:(c + 1) * 128], start=True, stop=True))

                    def p2(it):
                        d, c = divmod(it, NCHG)
                        ic = g32[:, c, 2 * d:2 * d + 1]
                        B.op("act", lambda e: e.activation(out=wv[:, it, 0:1], in_=bc(it, 0), func=AF.Exp,
                                                           scale=-1.0, bias=ic))
                        B.op("act", lambda e: e.activation(out=wv[:, it, 1:2], in_=bc(it, 2), func=AF.Exp, bias=ic))
                        B.op("act", lambda e: e.activation(out=eb[:, it:it + 1], in_=bc(it, 0), func=AF.Exp))
                        B.op("act", lambda e: e.activation(out=eL[:, it:it + 1], in_=bc(it, 1), func=AF.Exp))
                        B.op("dve", lambda e: e.tensor_tensor(out=SM[:, it], in0=stp(it), in1=mk32[:, d, :],
                                                              op=ALU.mult))

                    def p3(it):
                        d, c = divmod(it, NCHG)
                        B.op("dve", lambda e: e.tensor_scalar(out=vw[:, it], in0=vx[:, c], scalar1=wv[:, it, 0:1],
                                                              scalar2=None, op0=ALU.mult))
                        B.op("pool", lambda e: e.tensor_scalar(out=vwL[:, it], in0=vx[:, c], scalar1=wv[:, it, 1:2],
                                                               scalar2=None, op0=ALU.mult))

                    pipeline(B, [p1, p2, p3], NI)
                    B.step()
                with sbt("mC", [128, 2, 2, 129], F32) as C32, sbt("mCb", [128, 2, 129], BF16) as Cb, \
                        sbt("mho", [128, 2, 6, 129], F32) as ho, sbt("mdn", [128, 2, 6], F32) as dn, \
                        sbt("mhh", [128, 2, 6, 128], BF16) as hh:
                    B.op("dve", lambda e: e.memset(C32[:], 0.0))
                    B.op("pool", lambda e: e.memset(Cb[:], 0.0))
                    B.step()
                    border = list(range(NCc - 1, -1, -1)) + list(range(NCHG - 1, NCc - 1, -1))
                    order = [list(range(NCHG)), border]
                    for k in range(NCHG + 5):
                        if k < NCHG:
                            for d in range(2):
                                c = order[d][k]
                                it = d * NCHG + c
                                B.op("pe", lambda e, d=d, c=c: e.matmul(bank(d, 129), lhsT=qT[:, c * 128:(c + 1) * 128],
                                                                       rhs=Cb[:, d], start=True, stop=False))
                                B.op("pe", lambda e, d=d, it=it: e.matmul(bank(d, 129), lhsT=SM[:, it], rhs=vw[:, it],
                                                                         start=False, stop=True))
                                B.op("pe", lambda e, d=d, c=c, it=it: e.matmul(bank(2 + d, 129), lhsT=ktok[:, c],
                                                                              rhs=vwL[:, it], start=True, stop=True))
                            B.step()
                        for d in range(2):
                            if k < NCHG:
                                c = order[d][k]
                                it = d * NCHG + c
                                for out_ in (C32[:, d, (k + 1) % 2], Cb[:, d]):
                                    B.op("dve", lambda e, d=d, it=it, out_=out_: e.scalar_tensor_tensor(
                                        out=out_, in0=C32[:, d, k % 2], scalar=eL[:, it:it + 1], in1=bank(2 + d, 129),
                                        op0=ALU.mult, op1=ALU.add))
                                B.op("act", lambda e, d=d, it=it: e.activation(
                                    out=ho[:, d, k % 6], in_=bank(d, 129), func=AF.Copy, scale=eb[:, it:it + 1]))
                            k1, k2, k3, k4, k5 = k - 1, k - 2, k - 3, k - 4, k - 5
                            if 0 <= k1 < NCHG:
                                B.op("dve", lambda e, d=d, k1=k1: e.scalar_tensor_tensor(
                                    out=dn[:, d, k1 % 6:k1 % 6 + 1], in0=ho[:, d, k1 % 6, 128:129], scalar=-1.0,
                                    in1=ho[:, d, k1 % 6, 128:129], op0=ALU.mult, op1=ALU.max))
                            if 0 <= k2 < NCHG:
                                B.op("dve", lambda e, d=d, k2=k2: e.tensor_scalar(
                                    out=dn[:, d, k2 % 6:k2 % 6 + 1], in0=dn[:, d, k2 % 6:k2 % 6 + 1], scalar1=1.0,
                                    scalar2=None, op0=ALU.max))
                            if 0 <= k3 < NCHG:
                                B.op("dve", lambda e, d=d, k3=k3: e.reciprocal(
                                    out=dn[:, d, k3 % 6:k3 % 6 + 1], in_=dn[:, d, k3 % 6:k3 % 6 + 1]))
                            if 0 <= k4 < NCHG:
                                B.op("dve", lambda e, d=d, k4=k4: e.tensor_scalar(
                                    out=hh[:, d, k4 % 6], in0=ho[:, d, k4 % 6, 0:128], scalar1=dn[:, d, k4 % 6:k4 % 6 + 1],
                                    scalar2=None, op0=ALU.mult))
                            if 0 <= k5 < NCHG:
                                c5 = order[d][k5]
                                for dst in ret_dests(c5, d * 128, 128):
                                    B.dma("pool", dst, hh[:, d, k5 % 6])
                        B.step()

        def attn_phase():
            sel = rcv_sel()
            with sbt("aq", [128, NT], BF16) as qT, sbt("ak", [128, NT], BF16) as kT, \
                    sbt("avx", [128, NCHG, 257], BF16) as vx, sbt("apT", [128, 2, 1024], BF16) as pT, \
                    sbt("arec", [128, 2, 2], F32) as rec, sbt("aao", [128, 2, 2, 256], BF16) as ao:
                load_fm(sel, qT, 2 * T)
                load_fm(sel, kT, 3 * T)
                load_tm(sel, vx, 256, 256)
                B.op("pool", lambda e: e.memset(vx[:, :, 256:257], 1.0))
                B.step()
                items = []
                qtiles = [(0, list(range(NCc)))] + [(TC + q * 256, list(range(NCHG))) for q in range(cfg.S // 256)]
                for qi, (q0, chunks) in enumerate(qtiles):
                    grps = [chunks[k:k + 4] for k in range(0, len(chunks), 4)]
                    for gi, gch in enumerate(grps):
                        items.append((qi, q0, gch, gi == 0, gi == len(grps) - 1))
                scale = 128 ** -0.5

                def psS(it, ci):
                    o = (it % 2) * 1024 + ci * 256
                    return ps[:, o:o + 256]

                def acc(qi, sub):
                    return bank(4 + (qi % 2) * 2 + sub, 257)

                def sA(it):
                    qi, q0, gch, first, last = items[it]
                    for ci, kc in enumerate(gch):
                        B.op("pe", lambda e, ci=ci, kc=kc: e.matmul(psS(it, ci), lhsT=kT[:, kc * 128:(kc + 1) * 128],
                                                                   rhs=qT[:, q0:q0 + 256], start=True, stop=True))

                def sB(it):
                    qi, q0, gch, first, last = items[it]
                    n = len(gch) * 256
                    o = (it % 2) * 1024
                    B.op("act", lambda e: e.activation(out=pT[:, it % 2, 0:n], in_=ps[:, o:o + n], func=AF.Exp,
                                                       scale=scale))

                def sC(it):
                    qi, q0, gch, first, last = items[it]
                    for ci, kc in enumerate(gch):
                        for sub in range(2):
                            B.op("pe", lambda e, ci=ci, kc=kc, sub=sub: e.matmul(
                                acc(qi, sub), lhsT=pT[:, it % 2, ci * 256 + sub * 128:ci * 256 + (sub + 1) * 128],
                                rhs=vx[:, kc, :], start=(first and ci == 0), stop=(last and ci == len(gch) - 1)))

                def sD(it):
                    qi, q0, gch, first, last = items[it]
                    if last:
                        for sub in range(2):
                            B.op("dve", lambda e, sub=sub: e.reciprocal(out=rec[:, qi % 2, sub:sub + 1],
                                                                        in_=acc(qi, sub)[:, 256:257]))

                def sE(it):
                    qi, q0, gch, first, last = items[it]
                    if last:
                        for sub in range(2):
                            B.op("dve", lambda e, sub=sub: e.tensor_scalar(
                                out=ao[:, qi % 2, sub], in0=acc(qi, sub)[:, 0:256], scalar1=rec[:, qi % 2, sub:sub + 1],
                                scalar2=None, op0=ALU.mult))

                def sF(it):
                    qi, q0, gch, first, last = items[it]
                    if last:
                        for sub in range(2):
                            c = q0 // 128 + sub
                            for dst in ret_dests(c, 256, 256):
                                B.dma("pool", dst, ao[:, qi % 2, sub])

                pipeline(B, [sA, sB, sC, sD, sE, sF], len(items))
                B.step()

        def gmlp_phase(l, yaT):
            Rg = 12
            with sbt("guT", [128, 8, T], BF16) as uT, sbt("gws32", [128, 8, 128], F32) as ws32, \
                    sbt("gwsb", [128, 8, 128], BF16) as wsb, sbt("gbst", [128, 1024], F32) as bst, \
                    sbt("ggam", [128, 1024], F32) as gam, sbt("gv", [128, 4, 1024], F32) as v, \
                    sbt("gsq", [128, 2, 1024], F32) as sq, sbt("gst", [128, Rg, 2], F32) as stt, \
                    sbt("gvn", [128, 3, 1024], BF16) as vnb, sbt("gmx", [128, 2, 1024], F32) as mx:
                B.dma("sync", uT[:], uTd.ap().rearrange("(g p) t -> p g t", p=128))
                B.dma("sync", ws32[:], gws[:, l])
                B.dma("sync", bst[:], gbs[:, l])
                B.dma("sync", gam[:], gng[:, l])
                B.step()
                B.op("dve", lambda e: e.tensor_copy(out=wsb[:], in_=ws32[:]))
                B.step()

                def s0(c):
                    B.dma("sync", v[:, c % 4], vgd[c * 128:(c + 1) * 128, :])

                def s1(c):
                    B.op("dve", lambda e: e.reduce_sum(out=stt[:, c % Rg, 0:1], in_=v[:, c % 4], axis=AX.X))

                def s2(c):
                    B.op("dve", lambda e: e.tensor_scalar(out=stt[:, c % Rg, 0:1], in0=stt[:, c % Rg, 0:1],
                                                          scalar1=-1.0 / 1024, scalar2=None, op0=ALU.mult))

                def s3(c):
                    B.op("act", lambda e: e.activation(out=v[:, c % 4], in_=v[:, c % 4], func=AF.Identity,
                                                       bias=stt[:, c % Rg, 0:1]))

                def s4(c):
                    B.op("act", lambda e: e.activation(out=sq[:, c % 2], in_=v[:, c % 4], func=AF.Square))

                def s5(c):
                    B.op("dve", lambda e: e.reduce_sum(out=stt[:, c % Rg, 1:2], in_=sq[:, c % 2], axis=AX.X))

                def s6(c):
                    B.op("act", lambda e: e.activation(out=stt[:, c % Rg, 1:2], in_=stt[:, c % Rg, 1:2], func=AF.Sqrt,
                                                       scale=1.0 / 1024, bias=epsc[:, 0:1]))

                def s7(c):
                    B.op("dve", lambda e: e.reciprocal(out=stt[:, c % Rg, 1:2], in_=stt[:, c % Rg, 1:2]))

                def s8(c):
                    B.op("dve", lambda e: e.scalar_tensor_tensor(out=vnb[:, c % 3], in0=v[:, c % 4],
                                                                 scalar=stt[:, c % Rg, 1:2], in1=gam[:],
                                                                 op0=ALU.mult, op1=ALU.mult))

                def s9(c):
                    for g in range(8):
                        B.op("pe", lambda e, g=g: e.matmul(
                            ps[:, (c % 2) * 1024 + g * 128:(c % 2) * 1024 + (g + 1) * 128],
                            lhsT=vnb[:, c % 3, g * 128:(g + 1) * 128], rhs=wsb[:, g, :], start=True, stop=True))

                def s10(c):
                    B.op("dve", lambda e: e.tensor_tensor(out=mx[:, c % 2], in0=ps[:, (c % 2) * 1024:(c % 2 + 1) * 1024],
                                                          in1=bst[:], op=ALU.add))

                def s11(c):
                    B.op("dve", lambda e: e.tensor_tensor(
                        out=yaT[:, :, c * 128:(c + 1) * 128], in0=mx[:, c % 2].rearrange("p (g t) -> p g t", g=8),
                        in1=uT[:, :, c * 128:(c + 1) * 128], op=ALU.mult))

                sequential(B, [s0, s1, s2, s3, s4, s5, s6, s7, s8, s9, s10, s11], NCH)
                B.step()

        def post_phase(l, ybT, ycT, lam_init):
            Rg = 12
            rsel = myret.ap().rearrange("(j t) c -> t j c", j=8)
            with sbt("pdl", [128, 4, 128], F32) as dl, sbt("plam", [128, 4], F32) as lamt, \
                    sbt("pmng", [128, 1024], F32) as mg, sbt("pdng", [128, 1024], F32) as dg, \
                    sbt("prr", [128, 2, 8, 512], BF16) as rr, sbt("pog", [128, 2, 1024], BF16) as og, \
                    sbt("phs", [128, 4, 1024], F32) as hs, sbt("pod", [128, 4, 1024], F32) as od, \
                    sbt("psq", [128, 2, 2, 1024], F32) as sq, sbt("pst", [128, Rg, 2, 4], F32) as stt, \
                    sbt("pyb", [128, 2, 2, 1024], BF16) as ybc:
                B.dma("sync", dl[:], dlam[:, l])
                B.dma("sync", mg[:], mng[:, l])
                B.dma("sync", dg[:], dng[:, l])
                B.step()
                B.op("dve", lambda e: e.tensor_tensor(out=dl[:, 0], in0=dl[:, 0], in1=dl[:, 1], op=ALU.mult))
                B.op("pool", lambda e: e.tensor_tensor(out=dl[:, 2], in0=dl[:, 2], in1=dl[:, 3], op=ALU.mult))
                B.step()
                B.op("dve", lambda e: e.reduce_sum(out=lamt[:, 0:1], in_=dl[:, 0], axis=AX.X))
                B.op("dve", lambda e: e.reduce_sum(out=lamt[:, 1:2], in_=dl[:, 2], axis=AX.X))
                B.step()
                B.op("act", lambda e: e.activation(out=lamt[:, 0:2], in_=lamt[:, 0:2], func=AF.Exp))
                B.step()
                B.op("dve", lambda e: e.scalar_tensor_tensor(out=lamt[:, 2:3], in0=lamt[:, 1:2], scalar=-float(lam_init),
                                                             in1=lamt[:, 0:1], op0=ALU.add, op1=ALU.subtract))
                B.step()

                def s0(c):
                    B.dma("sync", rr[:, c % 2], rsel[c * 128:(c + 1) * 128])
                    B.dma("sync", og[:, c % 2], ogd[c * 128:(c + 1) * 128, :])

                def s1(c):
                    B.op("dve", lambda e: e.tensor_tensor(
                        out=hs[:, c % 4].rearrange("p (j e) -> p j e", j=8), in0=rr[:, c % 2, :, 0:128],
                        in1=rr[:, c % 2, :, 128:256], op=ALU.add))
                    r4 = rr[:, c % 2].rearrange("p (h m) c -> p h m c", m=2)
                    B.op("dve", lambda e: e.scalar_tensor_tensor(
                        out=od[:, c % 4].rearrange("p (h e) -> p h e", h=4), in0=r4[:, :, 1, 256:512],
                        scalar=lamt[:, 2:3], in1=r4[:, :, 0, 256:512], op0=ALU.mult, op1=ALU.add))

                def s2(c):
                    B.op("act", lambda e: e.activation(out=sq[:, c % 2, 0], in_=hs[:, c % 4], func=AF.Square))
                    B.op("act", lambda e: e.activation(out=sq[:, c % 2, 1], in_=od[:, c % 4], func=AF.Square))

                def s3(c):
                    for k in range(2):
                        B.op("dve", lambda e, k=k: e.reduce_sum(
                            out=stt[:, c % Rg, k, :], in_=sq[:, c % 2, k].rearrange("p (h e) -> p h e", h=4), axis=AX.X))

                def s4(c):
                    B.op("act", lambda e: e.activation(out=stt[:, c % Rg], in_=stt[:, c % Rg], func=AF.Sqrt,
                                                       scale=1.0 / 256, bias=epsc[:, 0:1]))

                def s5(c):
                    B.op("dve", lambda e: e.reciprocal(out=stt[:, c % Rg], in_=stt[:, c % Rg]))

                def s6(c):
                    for h in range(4):
                        B.op("dve", lambda e, h=h: e.scalar_tensor_tensor(
                            out=hs[:, c % 4, h * 256:(h + 1) * 256], in0=hs[:, c % 4, h * 256:(h + 1) * 256],
                            scalar=stt[:, c % Rg, 0, h:h + 1], in1=mg[:, h * 256:(h + 1) * 256],
                            op0=ALU.mult, op1=ALU.mult))
                        B.op("dve", lambda e, h=h: e.scalar_tensor_tensor(
                            out=od[:, c % 4, h * 256:(h + 1) * 256], in0=od[:, c % 4, h * 256:(h + 1) * 256],
                            scalar=stt[:, c % Rg, 1, h:h + 1], in1=dg[:, h * 256:(h + 1) * 256],
                            op0=ALU.mult, op1=ALU.mult))

                def s7(c):
                    B.op("dve", lambda e: e.tensor_tensor(out=ybc[:, c % 2, 0], in0=hs[:, c % 4], in1=og[:, c % 2],
                                                          op=ALU.mult))
                    B.op("act", lambda e: e.activation(out=ybc[:, c % 2, 1], in_=od[:, c % 4], func=AF.Copy,
                                                       scale=float(1.0 - lam_init)))

                def s8(c):
                    for k in range(2):
                        for f8 in range(8):
                            o = (c % 2) * 2048 + k * 1024 + f8 * 128
                            B.op("pe", lambda e, k=k, f8=f8, o=o: e.matmul(
                                ps[:, o:o + 128], lhsT=ybc[:, c % 2, k, f8 * 128:(f8 + 1) * 128], rhs=ident_b,
                                start=True, stop=True))

                def s9(c):
                    o = (c % 2) * 2048
                    B.op("act", lambda e: e.activation(
                        out=ybT[:, :, c * 128:(c + 1) * 128], in_=ps[:, o:o + 1024].rearrange("p (f t) -> p f t", f=8),
                        func=AF.Identity))
                    B.op("dve", lambda e: e.tensor_copy(
                        out=ycT[:, :, c * 128:(c + 1) * 128],
                        in_=ps[:, o + 1024:o + 2048].rearrange("p (f t) -> p f t", f=8)))

                sequential(B, [s0, s1, s2, s3, s4, s5, s6, s7, s8, s9], NCH)
                B.step()

        def merge_phase(l, ys, yT):
            n = 3 * KC
            with sbt("mwb", [128, 4, 1024], BF16) as wb, sbt("mgt", [128, 4, T], BF16) as gt, \
                    sbt("mm", [128, 4, T], F32) as m, sbt("macc", [128, 2, T], F32) as acc:
                def s0(i):
                    f, b = divmod(i, 3)
                    B.dma("sync", wb[:, i % 4], wblock(l, f"wbr{b}", f))
                    B.dma("sync", gt[:, i % 4], gTd[b * D + f * 128:b * D + (f + 1) * 128, :])

                def s1(i):
                    f, b = divmod(i, 3)
                    mm_block(wb[:, i % 4], ys[b], 8, i % 2)

                def s2(i):
                    for (pa, o, w) in evac_tiles(i % 2):
                        B.op("dve", lambda e, pa=pa, o=o, w=w: e.tensor_tensor(
                            out=m[:, i % 4, o:o + w], in0=pa, in1=gt[:, i % 4, o:o + w], op=ALU.mult))

                def s3(i):
                    f, b = divmod(i, 3)
                    if b == 1:
                        B.op("pool", lambda e: e.tensor_tensor(out=acc[:, f % 2], in0=m[:, (i - 1) % 4], in1=m[:, i % 4],
                                                               op=ALU.add))
                    elif b == 2:
                        B.op("pool", lambda e: e.tensor_tensor(out=yT[:, f, :], in0=acc[:, f % 2], in1=m[:, i % 4],
                                                               op=ALU.add))

                pipeline(B, [s0, s1, s2, s3], n)
                B.step()

        def mixer(l):
            bind_layer(l)
            stop = getattr(cfg, "stop_after", None)
            lam_init = 0.8 - 0.6 * math.exp(-0.3 * l)
            if stop == "prep":
                return
            sub_mod(l, 1, 1.0)
            with sbt("hTm", [128, KC, T], BF16) as hT:
                norm_mod(xT, hT_out(hT, 1))
                if stop == "norm":
                    return
                proj_phase(l, hT, stop)
            B.step()
            if stop == "proj" or (stop and stop.startswith("p_")):
                return
            for j in range(8):
                B.collective("AllGather", [V.snd[j * 128:(j + 1) * 128, :].opt()],
                             [V.rcv[j * 1024:(j + 1) * 1024, :].opt()])
            B.step()
            if stop == "ag1":
                return
            rcv_copy()
            if stop == "copy1":
                return
            mlstm_phase()
            if stop == "mlstm":
                return
            attn_phase()
            B.step()
            if stop == "attn":
                return
            for r in range(8):
                B.collective("AllGather", [V.ret[r * T:(r + 1) * T, :].opt()],
                             [V.rrcv[r * 8 * T:(r + 1) * 8 * T, :].opt()])
            B.step()
            ret_copy()
            if stop == "ag2":
                return
            with sbt("yaT", [128, 8, T], BF16) as yaT, sbt("ybT", [128, 8, T], BF16) as ybT, \
                    sbt("ycT", [128, 8, T], BF16) as ycT:
                gmlp_phase(l, yaT)
                if stop == "gmlp":
                    return
                post_phase(l, ybT, ycT, lam_init)
                if dbg is not None and l == 0:
                    for b_, yy in enumerate([yaT, ybT, ycT]):
                        B.dma("sync", dbg[b_ * 1024:(b_ + 1) * 1024, :].rearrange("(g p) t -> p g t", p=128), yy[:])
                    B.step()
                if stop == "post":
                    return
                with sbt("yT", [128, KC, T], BF16) as yT:
                    merge_phase(l, [yaT, ybT, ycT], yT)
                    gemm_residual(l, "wout", yT, KC)

        with sbt("xcp", [128, 2, T], F32) as xcp, sbt("ctmp", [128, KC, 2], F32) as ctmp:
            x0v = xT0.ap().rearrange("(kc p) t -> p kc t", p=128)
            xv = xT.ap().rearrange("(kc p) t -> p kc t", p=128)
            pipeline(B, [lambda kc: B.dma("sync", xcp[:, kc % 2], x0v[:, kc, :]),
                         lambda kc: B.dma("pool", xv[:, kc, :], xcp[:, kc % 2])], KC)
            B.dma("sync", ctmp[:], cT[:])
            B.step()
            B.op("act", lambda e: e.activation(out=scb[:], in_=ctmp[:], func=AF.Silu))
            B.step()

        for l in range(L):
            prep_layer(l)
            modulation(l)
            ffn(l, 0, "w13a", "w2a")
            if cfg.mixer:
                mixer(l)
            ffn(l, 2, "w13b", "w2b")
        final_norm()
        B.finish()
    return P


def host_inputs(cfg, inp):
    D, F, T, KC, L, TC, TL = cfg.D, cfg.F, cfg.T, cfg.KC, cfg.L, cfg.TC, cfg.TL
    f32 = np.float32
    x = np.asarray(inp["x"], f32)[0]
    ctx = np.asarray(inp["ctx"], f32)[0]
    cvec = np.asarray(inp["c"], f32)[0]
    cctx = np.asarray(inp["c_ctx"], f32)

    def pk(v):
        v = np.asarray(v, f32)
        lead = v.shape[:-1]
        a = v.reshape(lead + (KC, 128))
        return np.ascontiguousarray(np.moveaxis(a, -1, 0))

    consts = np.zeros((128, 4, 128), f32)
    consts[:, 0, :] = 1.0
    consts[:, 1, :] = np.eye(128, dtype=f32)
    consts[:, 2, :] = np.triu(np.ones((128, 128), f32))
    pm = np.zeros((128, 128), f32)
    for i in range(64):
        pm[2 * i + 1, 2 * i] = 1.0
        pm[2 * i, 2 * i + 1] = 1.0
    consts[:, 3, :] = pm
    cT = np.ascontiguousarray(np.stack([pk(cctx), pk(cvec)], axis=-1))
    shared = {
        "cT": cT, "consts": consts,
        "normg": pk(inp["norm_g"]),
        "finalg": pk(inp["final_g"]),
        "adab": np.ascontiguousarray(np.moveaxis(np.asarray(inp["ada_bias"], f32).reshape(L, 9 * KC, 128), -1, 0)),
        "adown": np.ascontiguousarray(np.asarray(inp["ada_down"], f32).reshape(L * D, cfg.R)),
    }
    if cfg.mixer:
        wi_all = np.asarray(inp["w_in"], f32)
        bc = lambda a: np.ascontiguousarray(np.broadcast_to(np.asarray(a, f32)[None], (128,) + tuple(np.shape(a))))
        shared["wg16"] = np.ascontiguousarray(
            np.moveaxis(wi_all[:, :, 5120:5136].reshape(L, KC, 128, 16), 2, 0))
        shared["gateb"] = bc(np.asarray(inp["mlstm_gate_b"], f32).reshape(L, 16))
        shared["gng"] = bc(inp["gmlp_norm_g"])
        shared["mng"] = bc(inp["mlstm_norm_g"])
        shared["dng"] = bc(np.tile(np.asarray(inp["diff_norm_g"], f32), (1, 4)))
        shared["gbs"] = bc(np.asarray(inp["gmlp_bs"], f32).reshape(L, 1024))
        shared["gws"] = np.ascontiguousarray(np.transpose(np.asarray(inp["gmlp_ws"], f32), (3, 0, 1, 2)))
        shared["dlam"] = bc(inp["diff_lambda"])
        tri = np.triu(np.ones((128, 128), f32))
        shared["masks"] = np.ascontiguousarray(np.stack([tri, tri.T, 1 - tri, 1 - tri.T], axis=1))
        n_freq = 32
        inv = (10000.0 ** (-np.arange(n_freq, dtype=np.float32) / n_freq)).astype(f32)
        tpos = np.arange(cfg.S)
        ang = np.concatenate([(tpos // 64).astype(f32)[:, None] * inv, (tpos % 64).astype(f32)[:, None] * inv],
                             axis=-1).astype(f32)
        cosf = np.repeat(np.cos(ang), 2, axis=1).T.astype(f32)
        sinf = np.repeat(np.sin(ang), 2, axis=1).T.astype(f32)
        sinf[0::2] *= -1.0
    wmats = {}
    for l in range(L):
        wmats[f"adaup_{l}"] = inp["ada_up"][l]
        wmats[f"w13a_{l}"] = inp["ffn_w13"][l, 0]
        wmats[f"w2a_{l}"] = inp["ffn_w2"][l, 0]
        wmats[f"w13b_{l}"] = inp["ffn_w13"][l, 1]
        wmats[f"w2b_{l}"] = inp["ffn_w2"][l, 1]
        if cfg.mixer:
            wi = np.asarray(inp["w_in"][l])
            wmats[f"win_{l}"] = np.concatenate([wi[:, :5120], wi[:, 5136:]], axis=1)
            for b in range(3):
                wmats[f"wbr{b}_{l}"] = inp["w_branch"][l, b]
            wmats[f"wout_{l}"] = inp["w_out"][l]
    maps = []
    for r in range(NCORES):
        m = dict(shared)
        xt = np.concatenate([ctx, x[r * TL:(r + 1) * TL]], axis=0)
        m["xT0"] = np.ascontiguousarray(xt.T)
        if cfg.mixer:
            m["ropet"] = np.ascontiguousarray(np.stack([cosf[:, r * TL:(r + 1) * TL], sinf[:, r * TL:(r + 1) * TL]], axis=1))
        for nm, w in wmats.items():
            n = w.shape[1] // NCORES
            m[nm] = np.ascontiguousarray(np.asarray(w[:, r * n:(r + 1) * n], f32))
        maps.append(m)
    return maps


def run(cfg, inp):
    P = build(cfg)
    print("BUILD: steps", P.B.nsteps, "instr", P.B.ninstr, "pairs used", P.B.pair + 1, flush=True)
    maps = host_inputs(cfg, inp)
    for m in maps:
        for k, (shp, dt) in P.inputs.items():
            assert tuple(m[k].shape) == shp, (k, m[k].shape, shp)
    maps = [{k: m[k] for k in P.inputs} for m in maps]
    res = run_bass_kernel_spmd(P.nc, maps, core_ids=list(range(NCORES)))
    outs = [np.asarray(res.results[r]["outT"]).T for r in range(NCORES)]
    if getattr(cfg, "debug", False):
        cfg.dbg_out = [np.asarray(res.results[r]["dbg"]).astype(np.float32) for r in range(NCORES)]
    return np.concatenate(outs, axis=0)[None].astype(np.float32)


def kernel(**inputs):
    cfg = Cfg()
    return run(cfg, inputs)
```
